# Optimizing a Trainium2 kernel written in Bass

```python
import jax, jax.numpy as jnp
from jax import lax
import numpy as np

D_MODEL = 1024
BATCH = 1
SEQ = 16384
DEPTH = 4

MLA_HEADS = 8
QK_NOPE_DIM = 64
QK_ROPE_DIM = 32
V_HEAD_DIM = 64
Q_LORA_RANK = 384
KV_LORA_RANK = 256
ROPE_THETA = 10000.0
Q_BLOCK = 128
SG_GROUPS = 8
SG_GROUP_DIM = 64
SG_WIDTH = SG_GROUPS * SG_GROUP_DIM
SG_CHUNK = 128
CONV_WIDTH = D_MODEL
CONV_K = 3
D_FF = 2816
NORM_EPS = 1e-6

MLA_OUT = MLA_HEADS * V_HEAD_DIM
MIX_WIDTH = MLA_OUT + SG_WIDTH
QK_HEAD_DIM = QK_NOPE_DIM + QK_ROPE_DIM
EVEN_IN = Q_LORA_RANK + KV_LORA_RANK + QK_ROPE_DIM + 2 * SG_WIDTH
N_EVEN = (DEPTH + 1) // 2
N_ODD = DEPTH // 2

kernel_name = "macaron_mla_sgu_shortconv_hybrid"


def rms_norm(x, g):
    x32 = x.astype(jnp.float32)
    y = x32 * lax.rsqrt(jnp.mean(x32 * x32, axis=-1, keepdims=True) + NORM_EPS)
    return (y * g.astype(jnp.float32)).astype(x.dtype)


def swiglu(h, w_gate, w_up, w_down):
    return (jax.nn.silu(h @ w_gate) * (h @ w_up)) @ w_down


def rope_tables(positions):
    inv_freq = ROPE_THETA ** (-jnp.arange(0, QK_ROPE_DIM, 2, dtype=jnp.float32) / QK_ROPE_DIM)
    ang = positions.astype(jnp.float32)[..., None] * inv_freq
    return jnp.cos(ang), jnp.sin(ang)


def apply_rope(t, cos, sin):
    t32 = t.astype(jnp.float32)
    t1, t2 = jnp.split(t32, 2, axis=-1)
    out = jnp.concatenate([t1 * cos - t2 * sin, t1 * sin + t2 * cos], axis=-1)
    return out.astype(t.dtype)


def mla_attention(q_nope, q_rope, k_nope, k_rope, v):
    B, S, H, _ = q_nope.shape
    nb = S // Q_BLOCK
    scale = QK_HEAD_DIM ** -0.5
    k_idx = jnp.arange(S)

    def to_blocks(t):
        return jnp.moveaxis(t.reshape(B, nb, Q_BLOCK, *t.shape[2:]), 1, 0)

    def one_block(args):
        qn, qr, i = args
        s = (jnp.einsum('bqhd,bkhd->bhqk', qn, k_nope, preferred_element_type=jnp.float32)
             + jnp.einsum('bqhr,bkr->bhqk', qr, k_rope, preferred_element_type=jnp.float32)) * scale
        q_idx = i * Q_BLOCK + jnp.arange(Q_BLOCK)
        s = jnp.where(k_idx[None, :] <= q_idx[:, None], s, -jnp.inf)
        p = jax.nn.softmax(s, axis=-1).astype(v.dtype)
        return jnp.einsum('bhqk,bkhd->bqhd', p, v)

    out = lax.map(one_block, (to_blocks(q_nope), to_blocks(q_rope), jnp.arange(nb)))
    return jnp.moveaxis(out, 0, 1).reshape(B, S, H * V_HEAD_DIM)


def spatial_gating(z, sg_norm, sg_w, sg_b):
    u, v = jnp.split(z, 2, axis=-1)
    v = rms_norm(v, sg_norm)
    B, S, _ = v.shape
    nc = S // SG_CHUNK
    v = v.reshape(B, nc, SG_CHUNK, SG_GROUPS, SG_GROUP_DIM)
    w = sg_w * jnp.tril(jnp.ones((SG_CHUNK, SG_CHUNK), dtype=sg_w.dtype))
    mixed = jnp.einsum('gts,bnsgc->bntgc', w, v) + sg_b.T[None, None, :, :, None]
    return u * mixed.reshape(B, S, SG_WIDTH)


def mla_sgu_mixer(h, cos, sin, w_in, q_norm, w_uq, kv_norm, w_ukv, sg_norm, sg_w, sg_b, w_out):
    B, S, _ = h.shape
    proj = h @ w_in
    c_q, c_kv, k_rope, z = jnp.split(
        proj, [Q_LORA_RANK, Q_LORA_RANK + KV_LORA_RANK, Q_LORA_RANK + KV_LORA_RANK + QK_ROPE_DIM], axis=-1)
    q = (rms_norm(c_q, q_norm) @ w_uq).reshape(B, S, MLA_HEADS, QK_HEAD_DIM)
    q_nope = q[..., :QK_NOPE_DIM]
    q_rope = apply_rope(q[..., QK_NOPE_DIM:], cos[:, :, None, :], sin[:, :, None, :])
    k_rope = apply_rope(k_rope, cos, sin)
    kv = (rms_norm(c_kv, kv_norm) @ w_ukv).reshape(B, S, MLA_HEADS, QK_NOPE_DIM + V_HEAD_DIM)
    k_nope, v = kv[..., :QK_NOPE_DIM], kv[..., QK_NOPE_DIM:]
    attn = mla_attention(q_nope, q_rope, k_nope, k_rope, v)
    sg = spatial_gating(jax.nn.gelu(z, approximate=False), sg_norm, sg_w, sg_b)
    return jnp.concatenate([attn, sg], axis=-1) @ w_out


def short_conv_mixer(h, w_in, conv_w, w_out):
    b_gate, c_gate, z = jnp.split(h @ w_in, 3, axis=-1)
    y = lax.conv_general_dilated(
        c_gate * z, conv_w[:, None, :], window_strides=(1,), padding=[(CONV_K - 1, 0)],
        dimension_numbers=('NWC', 'WIO', 'NWC'), feature_group_count=CONV_WIDTH)
    return (b_gate * y) @ w_out


def setup_inputs(seed: int = 0) -> dict:
    key = jax.random.key(seed)
    keys = iter(jax.random.split(key, 32))
    f32 = jnp.float32

    def dense(shape, fan_in):
        return jax.random.normal(next(keys), shape, f32) * (fan_in ** -0.5)

    def gain(shape):
        return 1.0 + 0.1 * jax.random.normal(next(keys), shape, f32)

    x = jax.random.normal(next(keys), (BATCH, SEQ, D_MODEL), f32)
    offset = jax.random.randint(next(keys), (BATCH, 1), 0, 1024, dtype=jnp.int32)
    positions = offset + jnp.arange(SEQ, dtype=jnp.int32)[None, :]
    return {
        "x": x,
        "positions": positions,
        "ffn_pre_norm": gain((DEPTH, D_MODEL)),
        "ffn_pre_w_gate": dense((DEPTH, D_MODEL, D_FF), D_MODEL),
        "ffn_pre_w_up": dense((DEPTH, D_MODEL, D_FF), D_MODEL),
        "ffn_pre_w_down": dense((DEPTH, D_FF, D_MODEL), D_FF),
        "mix_norm": gain((DEPTH, D_MODEL)),
        "ffn_post_norm": gain((DEPTH, D_MODEL)),
        "ffn_post_w_gate": dense((DEPTH, D_MODEL, D_FF), D_MODEL),
        "ffn_post_w_up": dense((DEPTH, D_MODEL, D_FF), D_MODEL),
        "ffn_post_w_down": dense((DEPTH, D_FF, D_MODEL), D_FF),
        "even_w_in": dense((N_EVEN, D_MODEL, EVEN_IN), D_MODEL),
        "q_norm": gain((N_EVEN, Q_LORA_RANK)),
        "w_uq": dense((N_EVEN, Q_LORA_RANK, MLA_HEADS * QK_HEAD_DIM), Q_LORA_RANK),
        "kv_norm": gain((N_EVEN, KV_LORA_RANK)),
        "w_ukv": dense((N_EVEN, KV_LORA_RANK, MLA_HEADS * (QK_NOPE_DIM + V_HEAD_DIM)), KV_LORA_RANK),
        "sg_norm": gain((N_EVEN, SG_WIDTH)),
        "sg_w": dense((N_EVEN, SG_GROUPS, SG_CHUNK, SG_CHUNK), SG_CHUNK),
        "sg_b": gain((N_EVEN, SG_GROUPS, SG_CHUNK)),
        "even_w_out": dense((N_EVEN, MIX_WIDTH, D_MODEL), MIX_WIDTH),
        "conv_w_in": dense((N_ODD, D_MODEL, 3 * CONV_WIDTH), D_MODEL),
        "conv_w": dense((N_ODD, CONV_K, CONV_WIDTH), CONV_K),
        "conv_w_out": dense((N_ODD, CONV_WIDTH, D_MODEL), CONV_WIDTH),
        "final_norm": gain((D_MODEL,)),
    }


def reference(x, positions, ffn_pre_norm, ffn_pre_w_gate, ffn_pre_w_up, ffn_pre_w_down,
              mix_norm, ffn_post_norm, ffn_post_w_gate, ffn_post_w_up, ffn_post_w_down,
              even_w_in, q_norm, w_uq, kv_norm, w_ukv, sg_norm, sg_w, sg_b, even_w_out,
              conv_w_in, conv_w, conv_w_out, final_norm):
    cos, sin = rope_tables(positions)
    for layer in range(DEPTH):
        x = x + 0.5 * swiglu(rms_norm(x, ffn_pre_norm[layer]),
                             ffn_pre_w_gate[layer], ffn_pre_w_up[layer], ffn_pre_w_down[layer])
        h = rms_norm(x, mix_norm[layer])
        if layer % 2 == 0:
            e = layer // 2
            x = x + mla_sgu_mixer(h, cos, sin, even_w_in[e], q_norm[e], w_uq[e], kv_norm[e],
                                  w_ukv[e], sg_norm[e], sg_w[e], sg_b[e], even_w_out[e])
        else:
            o = layer // 2
            x = x + short_conv_mixer(h, conv_w_in[o], conv_w[o], conv_w_out[o])
        x = x + 0.5 * swiglu(rms_norm(x, ffn_post_norm[layer]),
                             ffn_post_w_gate[layer], ffn_post_w_up[layer], ffn_post_w_down[layer])
    return rms_norm(x, final_norm)
```

```python
import math
import numpy as np
import concourse.bass as bass
import concourse.mybir as mybir
from concourse.bass_utils import run_bass_kernel_spmd

F32 = mybir.dt.float32
BF16 = mybir.dt.bfloat16
I32 = mybir.dt.int32
AF = mybir.ActivationFunctionType
ALU = mybir.AluOpType

NCORES = 8
SEQ = 16384
T = SEQ // NCORES
NT = T // 512
D = 1024
KC = 8
FF = 2816
NG = 11
DEPTH = 4
EPS = 1e-6
SLOT = 6144
NSLOT = 3
SCALE = 96 ** -0.5
TWO_PI = 2.0 * math.pi

C_PRE, C_MIX, C_POST, C_FIN = 0, 32, 64, 96
C_QN, C_KVN = 104, 110
C_CONVW = 114
C_INVF, C_PHQ, C_PHQ2, C_SGNB = 162, 163, 164, 165
C_SLOT, C_HSEL = 166, 174
NCV = 184

COMPUTE = ('pe', 'act', 'dve', 'pool')


class Ins:
    __slots__ = ('eng', 'fn', 'kind', 'deps', 'needs_inc', 'ticket', 'sem', 'val', 'prev', 'exempt')

    def __init__(self, eng, fn, kind):
        self.eng = eng
        self.fn = fn
        self.kind = kind
        self.deps = []
        self.needs_inc = False
        self.ticket = 0
        self.sem = None
        self.val = 0
        self.prev = 0
        self.exempt = False


class Prog:
    def __init__(self, nc):
        self.nc = nc
        self.streams = {e: [] for e in ('pe', 'act', 'dve', 'pool', 'sp')}
        self.lastw = {}
        self.readers = {}
        self.pending_fence = {}
        self.dmas_since = []
        self.all = []

    def add(self, eng, fn, reads=(), writes=(), kind='c', exempt=False):
        ins = Ins(eng, fn, kind)
        ins.exempt = exempt
        deps = {}
        for r in reads:
            w = self.lastw.get(r)
            if w is not None:
                deps[id(w)] = w
        for r in writes:
            w = self.lastw.get(r)
            if w is not None:
                deps[id(w)] = w
            rd = self.readers.get(r)
            if rd:
                for q in rd.values():
                    deps[id(q)] = q
        pf = self.pending_fence.pop(eng, None)
        if pf:
            for f in pf:
                deps[id(f)] = f
        ins.deps = list(deps.values())
        for r in reads:
            rd = self.readers.setdefault(r, {})
            if kind == 'c':
                rd[eng] = ins
            else:
                rd[id(ins)] = ins
        for r in writes:
            self.lastw[r] = ins
            self.readers[r] = {}
        self.streams[eng].append(ins)
        self.all.append(ins)
        if kind != 'c' and not exempt:
            self.dmas_since.append(ins)
        return ins

    def fence(self):
        fl = []
        for e in COMPUTE:
            for ins in reversed(self.streams[e]):
                if ins.kind == 'c':
                    fl.append(ins)
                    break
        fl += self.dmas_since
        self.dmas_since = []
        for e in self.streams:
            self.pending_fence[e] = list(self.pending_fence.get(e, [])) + fl

    def emit(self, sems):
        for ins in self.all:
            for d in ins.deps:
                if d.kind == 'c' and not (d.eng == 'pe' and ins.eng == 'pe'):
                    d.needs_inc = True
        for e in COMPUTE:
            c = 0
            for ins in self.streams[e]:
                if ins.kind == 'c' and ins.needs_inc:
                    c += 1
                    ins.ticket = c
        cci = 0
        for q, key in (('sp', 'dma_sp'), ('pool', 'dma_pool')):
            pool = sems[key]
            counts = [0] * len(pool)
            n = 0
            for ins in self.streams[q]:
                if ins.kind == 'd':
                    i = n % len(pool)
                    n += 1
                    ins.sem = pool[i]
                    ins.prev = counts[i]
                    counts[i] += 16
                    ins.val = counts[i]
                elif ins.kind == 'cc':
                    ins.sem = sems['cc'][cci]
                    cci += 1
                    ins.val = 1
        nc = self.nc
        prog = self

        def run_stream(name, e):
            waited = {}

            def need(sem, val):
                k = id(sem)
                if waited.get(k, 0) >= val:
                    return
                e.wait_ge(sem, val)
                waited[k] = val

            for ins in prog.streams[name]:
                for d in ins.deps:
                    if d.kind == 'c':
                        if d.eng == 'pe' and ins.eng == 'pe':
                            continue
                        need(sems[d.eng], d.ticket)
                    else:
                        need(d.sem, d.val)
                if ins.kind == 'd' and ins.prev > 0:
                    need(ins.sem, ins.prev)
                if ins.fn is None:
                    continue
                bi = ins.fn(e)
                if ins.kind == 'c':
                    if ins.needs_inc:
                        bi.then_inc(sems[ins.eng], 1)
                elif ins.kind == 'd':
                    bi.then_inc(ins.sem, 16)
                else:
                    bi.then_inc(ins.sem)

        with nc.Block() as block:
            @block.tensor
            def _(e):
                run_stream('pe', e)

            @block.scalar
            def _(e):
                run_stream('act', e)

            @block.vector
            def _(e):
                run_stream('dve', e)

            @block.gpsimd
            def _(e):
                run_stream('pool', e)

            @block.sync
            def _(e):
                run_stream('sp', e)


def build_program(n_layers=DEPTH, stop_step='c', final_norm=True, debug=False):
    nc = bass.Bass("TRN2", target_bir_lowering=False)
    dr = {}
    dbg_seen = {}

    def din(name, shape, dt=F32):
        dr[name] = nc.dram_tensor(name, list(shape), dt, kind="ExternalInput").ap()
        return dr[name]

    xT_d = din("xT", [128, KC, T])
    pos_d = din("pos", [128, T], I32)
    cvec_d = din("cvec", [128, NCV])
    tri_d = din("tri", [128, 128])
    wffn_d = din("wffn", [n_layers * 2 * NG + 1, 128, SLOT])
    we1_d = din("we1", [3, 128, 5120])
    we2_d = din("we2", [3, 128, 2048])
    we4_d = din("we4", [3, 128, 2048])
    wuq_d = din("wuq", [3, 128, 3072])
    weu_d = din("weu", [3, 128, 4096])
    wev_d = din("wev", [3, 128, 4096])
    wosg_d = din("wosg", [3, 128, 4096])
    woat_d = din("woat", [3, 128, 4096])
    sgwT_d = din("sgwT", [3, 128, 1024])
    sgb_d = din("sgb", [2, 128, 512])
    gsg_d = din("gsg", [2, 128, 512])
    wc1_d = din("wc1", [9, 128, SLOT])
    wc2_d = din("wc2", [5, 128, 4096])
    out_d = nc.dram_tensor("outT", [128, KC, T], F32, kind="ExternalOutput").ap()

    kin_d = [nc.dram_tensor(f"kin{e}", [8 * 128, T], BF16) for e in range(2)]
    kall_d = [nc.dram_tensor(f"kall{e}", [NCORES * 8 * 128, T], BF16) for e in range(2)]
    vin_d = [nc.dram_tensor(f"vin{e}", [8 * 128, 16 * 64], BF16) for e in range(2)]
    vall_d = [nc.dram_tensor(f"vall{e}", [NCORES * 8 * 128, 16 * 64], BF16) for e in range(2)]
    hin_d = [nc.dram_tensor(f"hin{o}", [128, 16], F32) for o in range(2)]
    hall_d = [nc.dram_tensor(f"hall{o}", [NCORES * 128, 16], F32) for o in range(2)]

    ARENA_BYTES = 212000
    arena_g = nc.sbuf_tensor("arena", [128, ARENA_BYTES // 2], BF16)
    arena = arena_g.__enter__()
    psum_g = nc.psum_tensor("ps", [128, 8, 512], F32)
    PS = psum_g.__enter__()

    def view(off, nbytes, dt, pattern=None, **kw):
        v = arena[:, off // 2:(off + nbytes) // 2]
        if dt == F32:
            v = v.bitcast(F32)
        elif dt == I32:
            v = v.bitcast(I32)
        if pattern:
            v = v.rearrange(pattern, **kw)
        return v

    X = view(0, 65536, F32, "p (k t) -> p k t", k=KC)
    TAB = view(65536, 16384, F32, "p (a t) -> p a t", a=2)
    RING = [view(81920 + i * SLOT * 2, SLOT * 2, BF16) for i in range(NSLOT)]
    CB = 118784
    CV = view(CB, 1024, F32)
    ONES = view(CB + 1024, 256, BF16)
    TRIB = view(CB + 1280, 256, BF16)
    TRIF = view(CB + 1536, 512, F32)
    SMALL = view(CB + 2048, 2048, F32)
    PH = CB + 4096
    SQ = [view(PH + i * 1024, 1024, BF16) for i in range(2)]
    RSTD = [view(PH + 2048 + i * 2048, 2048, F32) for i in range(2)]
    TMP = [view(PH + 6144 + i * 1024, 1024, F32) for i in range(2)]
    TMP = [view(PH + 6144, 2048, F32)]
    H = view(PH + 8192, 32768, BF16, "p (k t) -> p k t", k=KC)
    P2 = PH + 40960
    ACTB = [view(P2 + i * 8192, 8192, BF16, "p (f t) -> p f t", f=2) for i in range(2)]
    SILU = [view(P2 + 16384 + i * 1024, 1024, BF16) for i in range(3)]
    CRAW = view(P2, 20480, BF16, "p (k t) -> p k t", k=5)
    KR = view(P2 + 20480, 4096, BF16)
    KST = [view(P2 + 24576 + i * 4096, 4096, BF16) for i in range(2)]
    VST = [view(P2 + 32768 + i * 1024, 1024, BF16) for i in range(2)]
    T12 = [view(P2 + 34816 + i * 2048, 2048, F32) for i in range(2)]
    S2 = P2 + 12288
    UB = view(S2, 4096, BF16, "p (j t) -> p j t", j=4)
    SGB_ = view(S2 + 4096, 4096, BF16, "p (j t) -> p j t", j=4)
    VG = [view(S2 + 8192 + i * 2048, 2048, F32) for i in range(2)]
    VN = [view(S2 + 12288 + i * 1024, 1024, BF16) for i in range(2)]
    SGBIAS = view(S2 + 14336, 2048, F32)
    GSG = view(S2 + 16384, 2048, F32)
    WT = view(S2 + 18432, 2048, BF16, "p (g t) -> p g t", g=8)
    WT32 = view(S2 + 20480, 4096, F32, "p (g t) -> p g t", g=8)
    TMPS = view(S2 + 24576, 2048, F32)
    QH = [view(PH + 8192 + i * 4096, 4096, BF16) for i in range(2)]
    ATT = [view(PH + 16384 + i * 4096, 4096, BF16) for i in range(2)]
    KBUF = [view(PH + 24576 + i * 4096, 4096, BF16) for i in range(2)]
    VRAW = [view(PH + 32768 + i * 2048, 2048, BF16, "p (b d) -> p b d", b=16) for i in range(2)]
    ONES3 = view(PH + 36864, 2048, BF16, "p (b d) -> p b d", b=16)
    VBUF = [view(S2 + i * 4096, 4096, BF16, "p (b d) -> p b d", b=16) for i in range(2)]
    PT = [view(S2 + 8192 + i * 2048, 2048, BF16, "p (b q) -> p b q", b=2) for i in range(3)]
    RDEN = [view(S2 + 14336 + i * 2048, 2048, F32) for i in range(2)]
    WUQ = view(S2 + 18432, 6144, BF16, "p (k h c) -> p k h c", k=3, h=8)
    WOAT = view(S2 + 24576, 8192, BF16, "p (j c) -> p j c", j=4)
    assert S2 + 32768 <= ARENA_BYTES
    GATED = view(P2, 32768, BF16, "p (k t) -> p k t", k=KC)
    CZ = view(P2 + 32768, 8256, F32)
    YS = [view(P2 + 41024 + i * 2048, 2048, F32) for i in range(2)]
    assert P2 + 41024 + 4096 <= ARENA_BYTES, (P2 + 41024 + 4096)

    def sm(a, n):
        return SMALL[:, a:a + n]
    SS = [sm(0, 1), sm(1, 1)]
    ST_ = [sm(2, 1), sm(3, 1)]
    RS = [sm(4, 1), sm(5, 1)]
    BSAVE = sm(16, 16).rearrange("p (o t) -> p o t", o=8)
    YSAVE = sm(32, 16).rearrange("p (o t) -> p o t", o=8)
    HOUT = sm(48, 16).rearrange("p (o t) -> p o t", o=8)
    HZ = sm(64, 16).rearrange("p (o t) -> p o t", o=8)
    FIX = sm(80, 16).rearrange("p (o t) -> p o t", o=8)
    TMP8 = sm(96, 8)
    HALL = sm(128, 128).rearrange("p (r f) -> p r f", r=8)
    JUNK = view(PH + 6144, 2048, F32)

    P = Prog(nc)

    def dump(name, ap, reads, shape, dt):
        if not debug or name in dbg_seen:
            return
        t = nc.dram_tensor("dbg_" + name, list(shape), dt, kind="ExternalOutput").ap()
        dbg_seen[name] = t
        P.add('sp', lambda e: e.dma_start(out=t, in_=ap), reads=reads, writes=[('dbg', name)], kind='d')

    def cv(col):
        return CV[:, col:col + 1]

    ring_state = {'n': 0}

    def load_slab(src_ap, nelem):
        i = ring_state['n'] % NSLOT
        ring_state['n'] += 1
        dst = RING[i][:, 0:nelem]
        P.add('pool', lambda e, dst=dst, src=src_ap: e.dma_start(out=dst, in_=src),
              writes=[('ring', i)], kind='d', exempt=True)
        return i

    sched = []

    def plan():
        for l in range(n_layers):
            for which in (0, 1, 2):
                if which == 1:
                    if l % 2 == 0:
                        e = l // 2
                        sched.append((we1_d[e], 5120))
                        sched.append((we2_d[e], 2048))
                        sched.append((we4_d[e], 2048))
                        sched.append((weu_d[e], 4096))
                        sched.append((wev_d[e], 4096))
                        sched.append((wosg_d[e], 4096))
                    else:
                        o = l // 2
                        for s in range(4):
                            sched.append((wc1_d[o * 4 + s], SLOT))
                        for s in range(2):
                            sched.append((wc2_d[o * 2 + s], 4096))
                else:
                    li = l * 2 + (0 if which == 0 else 1)
                    for g in range(NG):
                        sched.append((wffn_d[li * NG + g], SLOT))
                if l == n_layers - 1 and which == {'a': 0, 'b': 1, 'c': 2}[stop_step]:
                    return
    plan()
    issued = {'n': 0, 'used': 0}
    slot_of = {}

    def prefetch(upto):
        while issued['n'] < min(upto, len(sched)):
            src, ne = sched[issued['n']]
            slot_of[issued['n']] = load_slab(src, ne)
            issued['n'] += 1

    def next_slab():
        i = issued['used']
        prefetch(i + 1)
        s = slot_of[i]
        issued['used'] += 1
        return s

    def after_use():
        prefetch(issued['used'] + 2)

    def mm(out, lhsT, rhs, start, stop, reads, writes):
        P.add('pe', lambda e, out=out, lhsT=lhsT, rhs=rhs, start=start, stop=stop:
              e.matmul(out, lhsT=lhsT, rhs=rhs, start=start, stop=stop), reads=reads, writes=writes)

    def act(out, in_, func, reads, writes, scale=1.0, bias=0.0, accum=None):
        if accum is None:
            P.add('act', lambda e: e.activation(out=out, in_=in_, func=func, scale=scale, bias=bias),
                  reads=reads, writes=writes)
        else:
            P.add('act', lambda e: e.activation(out=out, in_=in_, func=func, scale=scale, bias=bias,
                                                accum_out=accum), reads=reads, writes=writes)

    def dve(fn, reads, writes, eng='dve'):
        P.add(eng, fn, reads=reads, writes=writes)

    def xk(k, tt):
        return ('x', k, tt)

    def tsl(tt):
        return slice(tt * 512, (tt + 1) * 512)

    P.add('sp', lambda e: e.dma_start(out=CV[:, 0:NCV], in_=cvec_d), writes=[('cv',)], kind='d')
    P.add('sp', lambda e: e.dma_start(out=TRIF, in_=tri_d), writes=[('trif',)], kind='d')
    for k in range(KC):
        P.add('sp', lambda e, k=k: e.dma_start(out=X[:, k, :], in_=xT_d[:, k, :]),
              writes=[xk(k, tt) for tt in range(NT)], kind='d')
    prefetch(2)
    dve(lambda e: e.memset(ONES, 1.0), [], [('ones',)])
    dve(lambda e: e.tensor_copy(out=TRIB, in_=TRIF), [('trif',)], [('trib',)])

    has_even = n_layers >= 1
    if has_even:
        POSI = view(PH + 8192, 8192, I32)
        ANG = view(PH + 16384, 8192, F32)
        KF = view(PH + 24576, 8192, F32)
        KI = view(PH + 32768, 8192, I32)
        DD = view(PH + 40960, 8192, F32)
        P.add('sp', lambda e: e.dma_start(out=POSI, in_=pos_d), writes=[('posi',)], kind='d')
        dve(lambda e: e.tensor_copy(out=ANG, in_=POSI), [('posi',)], [('ang',)])
        dve(lambda e: e.tensor_scalar(out=ANG, in0=ANG, scalar1=cv(C_INVF), scalar2=None, op0=ALU.mult),
            [('ang',), ('cv',)], [('ang',)])
        dve(lambda e: e.tensor_scalar(out=KF, in0=ANG, scalar1=1.0 / TWO_PI, scalar2=None, op0=ALU.mult),
            [('ang',)], [('kf',)])
        dve(lambda e: e.tensor_copy(out=KI, in_=KF), [('kf',)], [('ki',)])
        dve(lambda e: e.tensor_copy(out=DD, in_=KI), [('ki',)], [('dd',)])
        C1 = 6.28125
        C2 = TWO_PI - C1
        dve(lambda e: e.scalar_tensor_tensor(out=ANG, in0=DD, scalar=-C1, in1=ANG, op0=ALU.mult, op1=ALU.add),
            [('dd',), ('ang',)], [('ang',)])
        dve(lambda e: e.scalar_tensor_tensor(out=ANG, in0=DD, scalar=-C2, in1=ANG, op0=ALU.mult, op1=ALU.add),
            [('dd',), ('ang',)], [('ang',)])
        dve(lambda e: e.tensor_scalar(out=KF, in0=ANG, scalar1=math.pi, scalar2=-TWO_PI, op0=ALU.is_gt, op1=ALU.mult),
            [('ang',)], [('kf',)])
        dve(lambda e: e.tensor_tensor(out=ANG, in0=ANG, in1=KF, op=ALU.add), [('ang',), ('kf',)], [('ang',)])
        dve(lambda e: e.tensor_scalar(out=KF, in0=ANG, scalar1=-math.pi, scalar2=TWO_PI, op0=ALU.is_lt, op1=ALU.mult),
            [('ang',)], [('kf',)])
        dve(lambda e: e.tensor_tensor(out=ANG, in0=ANG, in1=KF, op=ALU.add), [('ang',), ('kf',)], [('ang',)])
        for a, col in ((0, C_PHQ), (1, C_PHQ2)):
            dve(lambda e, col=col: e.tensor_scalar(out=DD, in0=ANG, scalar1=cv(col), scalar2=None, op0=ALU.add),
                [('ang',), ('cv',)], [('dd',)])
            dve(lambda e: e.tensor_scalar(out=KF, in0=DD, scalar1=math.pi, scalar2=-TWO_PI, op0=ALU.is_gt, op1=ALU.mult),
                [('dd',)], [('kf',)])
            dve(lambda e: e.tensor_tensor(out=DD, in0=DD, in1=KF, op=ALU.add), [('dd',), ('kf',)], [('dd',)])
            act(TAB[:, a, :], DD, AF.Sin, [('dd',)], [('tab', a)], scale=0.999999)
        dump('tab', TAB, [('tab', 0), ('tab', 1)], [128, 2, T], F32)
        P.fence()

    ss_bank = {'n': 0}

    def rmsnorm_to_H(gcol, in_place=False):
        for tt in range(NT):
            b = 6 + (ss_bank['n'] % 2)
            ss_bank['n'] += 1
            for k in range(KC):
                sq = SQ[k % 2]
                act(sq, X[:, k, tsl(tt)], AF.Square, [xk(k, tt)], [('sq', k % 2)])
                mm(PS[:, b, :], ONES, sq, k == 0, k == KC - 1, [('sq', k % 2), ('ones',)], [('ps', b)])
            act(TMP[0], PS[:, b, :], AF.Sqrt, [('ps', b)], [('tmp', 0)], scale=1.0 / D, bias=EPS)
            r = RSTD[tt % 2]
            dve(lambda e, r=r: e.reciprocal(out=r, in_=TMP[0]), [('tmp', 0)], [('rstd', tt % 2)])
            for k in range(KC):
                if in_place:
                    o = X[:, k, tsl(tt)]
                    wr = [xk(k, tt)]
                else:
                    o = H[:, k, tsl(tt)]
                    wr = [('h', k, tt)]
                dve(lambda e, o=o, k=k, tt=tt, r=r: e.scalar_tensor_tensor(
                    out=o, in0=X[:, k, tsl(tt)], scalar=cv(gcol + k), in1=r, op0=ALU.mult, op1=ALU.mult),
                    [xk(k, tt), ('rstd', tt % 2), ('cv',)], wr)

    def ffn(l, which):
        gcol = (C_PRE if which == 0 else C_POST) + l * 8
        rmsnorm_to_H(gcol)
        cnt = {'gu': 0, 'o': 0, 'silu': 0}

        def gate_up(g, slot):
            W = RING[slot]
            WGU = W[:, 0:4096].rearrange("p (k c) -> p k c", k=KC)
            ab = ACTB[g % 2]
            for fc in range(2):
                for tt in range(NT):
                    bg = (cnt['gu'] % 2) * 2
                    bu = bg + 1
                    cnt['gu'] += 1
                    for k in range(KC):
                        mm(PS[:, bg, :], WGU[:, k, fc * 128:(fc + 1) * 128], H[:, k, tsl(tt)], k == 0, k == KC - 1,
                           [('ring', slot), ('h', k, tt)], [('ps', bg)])
                    for k in range(KC):
                        mm(PS[:, bu, :], WGU[:, k, 256 + fc * 128:256 + (fc + 1) * 128], H[:, k, tsl(tt)], k == 0, k == KC - 1,
                           [('ring', slot), ('h', k, tt)], [('ps', bu)])
                    si = cnt['silu'] % 3
                    cnt['silu'] += 1
                    act(SILU[si], PS[:, bg, :], AF.Silu, [('ps', bg)], [('silu', si)])
                    dve(lambda e, ab=ab, fc=fc, tt=tt, si=si, bu=bu: e.tensor_tensor(
                        out=ab[:, fc, tsl(tt)], in0=SILU[si], in1=PS[:, bu, :], op=ALU.mult),
                        [('silu', si), ('ps', bu)], [('actb', g % 2, fc, tt)])

        def down(g, slot):
            W = RING[slot]
            WD = W[:, 4096:6144].rearrange("p (f c) -> p f c", f=2)
            ab = ACTB[g % 2]
            for tt in range(NT):
                for d in range(KC):
                    bo = 4 + (cnt['o'] % 2)
                    cnt['o'] += 1
                    for fc in range(2):
                        mm(PS[:, bo, :], WD[:, fc, d * 128:(d + 1) * 128], ab[:, fc, tsl(tt)], fc == 0, fc == 1,
                           [('ring', slot), ('actb', g % 2, fc, tt)], [('ps', bo)])
                    dve(lambda e, d=d, tt=tt, bo=bo: e.scalar_tensor_tensor(
                        out=X[:, d, tsl(tt)], in0=PS[:, bo, :], scalar=0.5, in1=X[:, d, tsl(tt)],
                        op0=ALU.mult, op1=ALU.add), [('ps', bo), xk(d, tt)], [xk(d, tt)])

        slots = []
        for g in range(NG):
            s = next_slab()
            slots.append(s)
            gate_up(g, s)
            if g >= 1:
                down(g - 1, slots[g - 1])
                after_use()
        down(NG - 1, slots[NG - 1])
        after_use()

    def even_mixer(l):
        e_ = l // 2
        rmsnorm_to_H(C_MIX + l * 8)
        s1 = next_slab()
        W1 = RING[s1][:, 0:5120].rearrange("p (k c) -> p k c", k=KC)
        pb = {'n': 0}
        for tt in range(NT):
            for oc in range(5):
                b = pb['n'] % 2
                pb['n'] += 1
                for k in range(KC):
                    mm(PS[:, b, :], W1[:, k, oc * 128:(oc + 1) * 128], H[:, k, tsl(tt)], k == 0, k == KC - 1,
                       [('ring', s1), ('h', k, tt)], [('ps', b)])
                act(CRAW[:, oc, tsl(tt)], PS[:, b, :], AF.Identity, [('ps', b)], [('craw', oc, tt)])
                sq = SQ[oc % 2]
                act(sq, PS[:, b, :], AF.Square, [('ps', b)], [('sq', oc % 2)])
                sb = 2 if oc < 3 else 3
                first = oc in (0, 3)
                last = oc in (2, 4)
                mm(PS[:, sb, :], ONES, sq, first, last, [('sq', oc % 2), ('ones',)], [('ps', sb)])
            for (sb, n, ocs, gc, ri) in ((2, 384, (0, 1, 2), C_QN + e_ * 3, 0), (3, 256, (3, 4), C_KVN + e_ * 2 - 3, 1)):
                act(TMP[0], PS[:, sb, :], AF.Sqrt, [('ps', sb)], [('tmp', 0)], scale=1.0 / n, bias=EPS)
                r = RSTD[ri]
                dve(lambda e, r=r: e.reciprocal(out=r, in_=TMP[0]), [('tmp', 0)], [('rstd', ri)])
                for oc in ocs:
                    dve(lambda e, oc=oc, tt=tt, r=r, gc=gc: e.scalar_tensor_tensor(
                        out=CRAW[:, oc, tsl(tt)], in0=CRAW[:, oc, tsl(tt)], scalar=cv(gc + oc), in1=r,
                        op0=ALU.mult, op1=ALU.mult), [('craw', oc, tt), ('rstd', ri), ('cv',)], [('craw', oc, tt)])
        after_use()
        s2 = next_slab()
        W2 = RING[s2][:, 0:2048].rearrange("p (k c) -> p k c", k=KC)
        for tt in range(NT):
            for k in range(KC):
                mm(PS[:, 4, :], W2[:, k, 0:128], H[:, k, tsl(tt)], k == 0, k == KC - 1,
                   [('ring', s2), ('h', k, tt)], [('ps', 4)])
            for k in range(KC):
                mm(PS[:, 5, :], W2[:, k, 128:256], H[:, k, tsl(tt)], k == 0, k == KC - 1,
                   [('ring', s2), ('h', k, tt)], [('ps', 5)])
            dve(lambda e, tt=tt: e.tensor_tensor(out=T12[0][64:128, :], in0=PS[64:128, 4, :], in1=TAB[64:128, 0, tsl(tt)],
                                                op=ALU.mult), [('ps', 4), ('tab', 0)], [('t12', 0)])
            dve(lambda e, tt=tt: e.scalar_tensor_tensor(out=T12[1][64:128, :], in0=PS[64:128, 5, :], scalar=CV[64:128, C_SGNB:C_SGNB + 1],
                                                       in1=TAB[64:128, 1, tsl(tt)], op0=ALU.mult, op1=ALU.mult),
                [('ps', 5), ('tab', 1), ('cv',)], [('t12', 1)])
            dve(lambda e, tt=tt: e.tensor_tensor(out=KR[64:128, tsl(tt)], in0=T12[0][64:128, :], in1=T12[1][64:128, :],
                                                op=ALU.add), [('t12', 0), ('t12', 1)], [('kr', tt)])
        after_use()
        s4 = next_slab()
        WUK = RING[s4][:, 0:1024].rearrange("p (k h c) -> p k h c", k=2, h=8)
        WUV = RING[s4][:, 1024:2048].rearrange("p (k c) -> p k c", k=2)
        kin = kin_d[e_].ap()
        vin = vin_d[e_].ap()
        for h in range(8):
            ks = KST[h % 2]
            for tt in range(NT):
                b = pb['n'] % 2
                pb['n'] += 1
                for k in range(2):
                    mm(PS[0:64, b, :], WUK[:, k, h, :], CRAW[:, 3 + k, tsl(tt)], k == 0, k == 1,
                       [('ring', s4), ('craw', 3 + k, tt)], [('ps', b)])
                act(ks[0:64, tsl(tt)], PS[0:64, b, :], AF.Identity, [('ps', b)], [('kst', h % 2, 0, tt)])
            P.add('pool', lambda e, ks=ks: e.tensor_copy(out=ks[64:128, :], in_=KR[64:128, :]),
                  reads=[('kr', tt) for tt in range(NT)], writes=[('kst', h % 2, 1)])
            dump('kst', ks, [('kst', h % 2, 0, tt) for tt in range(NT)] + [('kst', h % 2, 1)], [128, T], BF16)
            P.add('sp', lambda e, ks=ks, h=h: e.dma_start(out=kin[h * 128:(h + 1) * 128, :], in_=ks),
                  reads=[('kst', h % 2, 0, tt) for tt in range(NT)] + [('kst', h % 2, 1)],
                  writes=[('kin', e_, h)], kind='d')
        vin_v = vin.rearrange("(h p) (b d) -> p b h d", p=128, d=64)
        for blk in range(16):
            b = pb['n'] % 2
            pb['n'] += 1
            tt = blk // 4
            for k in range(2):
                mm(PS[:, b, :], CRAW[:, 3 + k, blk * 128:(blk + 1) * 128], WUV[:, k, :], k == 0, k == 1,
                   [('ring', s4), ('craw', 3 + k, tt)], [('ps', b)])
            vs = VST[blk % 2]
            act(vs, PS[:, b, :], AF.Identity, [('ps', b)], [('vst', blk % 2)])
            dump('vst', vs, [('vst', blk % 2)], [128, 512], BF16)
            P.add('sp', lambda e, vs=vs, blk=blk: e.dma_start(
                out=vin_v[:, blk, :, :], in_=vs.rearrange("p (h d) -> p h d", h=8)),
                reads=[('vst', blk % 2)], writes=[('vin', e_, blk)], kind='d')
        after_use()
        P.add('pool', lambda e: e.collective_compute(
            "AllGather", ALU.bypass, replica_groups=[list(range(NCORES))],
            ins=[kin_d[e_].ap().opt()], outs=[kall_d[e_].ap().opt()]),
            reads=[('kin', e_, h) for h in range(8)], writes=[('kall', e_)], kind='cc', exempt=True)
        P.add('pool', lambda e: e.collective_compute(
            "AllGather", ALU.bypass, replica_groups=[list(range(NCORES))],
            ins=[vin_d[e_].ap().opt()], outs=[vall_d[e_].ap().opt()]),
            reads=[('vin', e_, b) for b in range(16)], writes=[('vall', e_)], kind='cc', exempt=True)
        P.fence()
        su = next_slab()
        sv = next_slab()
        so = next_slab()
        WU = RING[su][:, 0:4096].rearrange("p (k c) -> p k c", k=KC)
        WV = RING[sv][:, 0:4096].rearrange("p (k c) -> p k c", k=KC)
        WO = RING[so][:, 0:4096].rearrange("p (j c) -> p j c", j=4)
        P.add('sp', lambda e: e.dma_start(out=WT32, in_=sgwT_d[e_].rearrange("p (g t) -> p g t", g=8)),
              writes=[('wt32',)], kind='d')
        P.add('sp', lambda e: e.dma_start(out=SGBIAS, in_=sgb_d[e_]), writes=[('sgbias',)], kind='d')
        P.add('sp', lambda e: e.dma_start(out=GSG, in_=gsg_d[e_]), writes=[('gsg',)], kind='d')
        for g in range(8):
            dve(lambda e, g=g: e.tensor_tensor(out=WT[:, g, :], in0=WT32[:, g, :], in1=TRIF, op=ALU.mult),
                [('wt32',), ('trif',)], [('wt', g)])
        for tt in range(NT):
            for j in range(4):
                b = pb['n'] % 2
                pb['n'] += 1
                for k in range(KC):
                    mm(PS[:, b, :], WU[:, k, j * 128:(j + 1) * 128], H[:, k, tsl(tt)], k == 0, k == KC - 1,
                       [('ring', su), ('h', k, tt)], [('ps', b)])
                act(UB[:, j, :], PS[:, b, :], AF.Gelu, [('ps', b)], [('ub', j)])
            for bi in range(4):
                blk = tt * 4 + bi
                i2 = blk % 2
                bv = 2 + i2
                for k in range(KC):
                    mm(PS[:, bv, :], H[:, k, blk * 128:(blk + 1) * 128], WV[:, k, :], k == 0, k == KC - 1,
                       [('ring', sv), ('h', k, tt)], [('ps', bv)])
                act(VG[i2], PS[:, bv, :], AF.Gelu, [('ps', bv)], [('vg', i2)])
                act(JUNK, VG[i2], AF.Square, [('vg', i2)], [('junk',), ('ss', i2)], accum=SS[i2])
                act(ST_[i2], SS[i2], AF.Sqrt, [('ss', i2)], [('st', i2)], scale=1.0 / 512, bias=EPS)
                dve(lambda e, i2=i2: e.reciprocal(out=RS[i2], in_=ST_[i2]), [('st', i2)], [('rs', i2)])
                dve(lambda e, i2=i2: e.scalar_tensor_tensor(out=VN[i2], in0=VG[i2], scalar=RS[i2], in1=GSG,
                                                            op0=ALU.mult, op1=ALU.mult),
                    [('vg', i2), ('rs', i2), ('gsg',)], [('vn', i2)])
                bm = 4 + i2
                for g in range(8):
                    mm(PS[(g % 2) * 64:(g % 2) * 64 + 64, bm, (g // 2) * 128:(g // 2) * 128 + 128],
                       VN[i2][:, g * 64:(g + 1) * 64], WT[:, g, :], True, True,
                       [('vn', i2), ('wt', g)], [('ps', bm)])
                dve(lambda e, bm=bm: e.tensor_tensor(out=TMPS, in0=PS[:, bm, :], in1=SGBIAS, op=ALU.add),
                    [('ps', bm), ('sgbias',)], [('tmps',)])
                dve(lambda e, bi=bi: e.tensor_tensor(
                    out=SGB_[:, :, bi * 128:(bi + 1) * 128], in0=TMPS.rearrange("p (j t) -> p j t", j=4),
                    in1=UB[:, :, bi * 128:(bi + 1) * 128], op=ALU.mult),
                    [('tmps',)] + [('ub', j) for j in range(4)], [('sgo', bi)])
            for d in range(KC):
                bo = 6 + (d % 2)
                for j in range(4):
                    mm(PS[:, bo, :], WO[:, j, d * 128:(d + 1) * 128], SGB_[:, j, :], j == 0, j == 3,
                       [('ring', so)] + [('sgo', bi) for bi in range(4)], [('ps', bo)])
                dve(lambda e, d=d, tt=tt, bo=bo: e.tensor_tensor(out=X[:, d, tsl(tt)], in0=PS[:, bo, :],
                                                                in1=X[:, d, tsl(tt)], op=ALU.add),
                    [('ps', bo), xk(d, tt)], [xk(d, tt)])
        after_use()
        dump('craw', CRAW, [('craw', oc, tt) for oc in range(3) for tt in range(NT)], [128, 5, T], BF16)
        P.fence()
        P.add('pool', lambda e: e.dma_start(out=WUQ.rearrange("p k h c -> p (k h c)"), in_=wuq_d[e_]),
              writes=[('wuq',)], kind='d')
        P.add('pool', lambda e: e.dma_start(out=WOAT.rearrange("p j c -> p (j c)"), in_=woat_d[e_]),
              writes=[('woat',)], kind='d')
        dve(lambda e: e.memset(ONES3, 1.0), [], [('ones3',)])
        kall = kall_d[e_].ap()
        vall = vall_d[e_].ap()
        ld = {'n': 0}
        sb_cnt = {'n': 0, 'pt': 0}
        for h in range(8):
            qh = QH[h % 2]
            for tt in range(NT):
                for k in range(3):
                    mm(PS[:, 4, :], WUQ[:, k, h, :], CRAW[:, k, tsl(tt)], k == 0, k == 2,
                       [('wuq',), ('craw', k, tt)], [('ps', 4)])
                act(qh[0:64, tsl(tt)], PS[0:64, 4, :], AF.Identity, [('ps', 4)], [('qh', h % 2, 0, tt)])
                dve(lambda e, qh=qh, tt=tt: e.tensor_tensor(out=qh[64:128, tsl(tt)], in0=PS[64:128, 4, :],
                                                            in1=TAB[64:128, 0, tsl(tt)], op=ALU.mult),
                    [('ps', 4), ('tab', 0)], [('qh', h % 2, 1, tt)])
            qreads = lambda qt: [('qh', h % 2, 0, qt), ('qh', h % 2, 1, qt)]
            dump('qh', qh, [('qh', h % 2, i_, tt) for i_ in range(2) for tt in range(NT)], [128, T], BF16)
            first_seen = set()
            n_acc = {qt: 7 * 16 + 4 * qt + 4 for qt in range(NT)}
            acc_cnt = {qt: 0 for qt in range(NT)}

            def load_slot(slot):
                li = ld['n'] % 2
                ld['n'] += 1
                if slot < 7:
                    ksrc = kall[(slot * 8 + h) * 128:(slot * 8 + h + 1) * 128, :]
                    vsrc = vall[(slot * 8 + h) * 128:(slot * 8 + h + 1) * 128, :]
                    kr_, vr_ = [('kall', e_)], [('vall', e_)]
                    mcol = C_SLOT + slot
                else:
                    ksrc = kin[h * 128:(h + 1) * 128, :]
                    vsrc = vin[h * 128:(h + 1) * 128, :]
                    kr_, vr_ = [('kin', e_, h)], [('vin', e_, b) for b in range(16)]
                    mcol = None
                P.add('sp', lambda e, li=li, ksrc=ksrc: e.dma_start(out=KBUF[li], in_=ksrc),
                      reads=kr_, writes=[('kbuf', li)], kind='d')
                P.add('sp', lambda e, li=li, vsrc=vsrc: e.dma_start(
                    out=VRAW[li].rearrange("p b d -> p (b d)"), in_=vsrc),
                    reads=vr_, writes=[('vraw', li)], kind='d')
                if mcol is not None:
                    P.add('pool', lambda e, li=li, mcol=mcol: e.tensor_scalar(
                        out=VBUF[li][:, :, 0:64], in0=VRAW[li], scalar1=cv(mcol), scalar2=0.0, op0=ALU.mult, op1=ALU.add),
                        reads=[('vraw', li), ('cv',)], writes=[('vbuf', li, 0)])
                    P.add('pool', lambda e, li=li, mcol=mcol: e.tensor_scalar(
                        out=VBUF[li][:, :, 64:128], in0=ONES3, scalar1=cv(mcol), scalar2=0.0, op0=ALU.mult, op1=ALU.add),
                        reads=[('ones3',), ('cv',)], writes=[('vbuf', li, 1)])
                else:
                    P.add('pool', lambda e, li=li: e.tensor_copy(out=VBUF[li][:, :, 0:64], in_=VRAW[li]),
                          reads=[('vraw', li)], writes=[('vbuf', li, 0)])
                    P.add('pool', lambda e, li=li: e.tensor_copy(out=VBUF[li][:, :, 64:128], in_=ONES3),
                          reads=[('ones3',)], writes=[('vbuf', li, 1)])
                return li

            def emit_S(unit):
                sp_ = 4 + 2 * (sb_cnt['n'] % 2)
                sb_cnt['n'] += 1
                for i_, bl in enumerate(unit):
                    li, kb, qt, c0, dg = bl
                    mm(PS[:, sp_ + i_, c0:512], KBUF[li][:, kb * 128:(kb + 1) * 128], qh[:, qt * 512 + c0:(qt + 1) * 512],
                       True, True, [('kbuf', li)] + qreads(qt), [('ps', sp_ + i_)])
                return sp_

            def emit_EP(unit, sp_):
                pi = sb_cnt['pt'] % 3
                sb_cnt['pt'] += 1
                pt = PT[pi]
                if len(unit) == 2:
                    act(pt, PS[:, sp_:sp_ + 2, :], AF.Exp, [('ps', sp_), ('ps', sp_ + 1)], [('pt', pi)], scale=SCALE)
                else:
                    li, kb, qt, c0, dg = unit[0]
                    act(pt[:, 0, c0:512], PS[:, sp_, c0:512], AF.Exp, [('ps', sp_)], [('pt', pi)], scale=SCALE)
                    if dg:
                        dve(lambda e, pt=pt, c0=c0: e.tensor_tensor(out=pt[:, 0, c0:c0 + 128], in0=pt[:, 0, c0:c0 + 128],
                                                                    in1=TRIB, op=ALU.mult),
                            [('pt', pi), ('trib',)], [('pt', pi)])
                for i_, bl in enumerate(unit):
                    li, kb, qt, c0, dg = bl
                    st = qt not in first_seen
                    first_seen.add(qt)
                    acc_cnt[qt] += 1
                    mm(PS[:, qt, c0:512], VBUF[li][:, kb, :], pt[:, i_, c0:512], st, acc_cnt[qt] == n_acc[qt],
                       [('vbuf', li, 0), ('vbuf', li, 1), ('pt', pi)], [('ps', qt)])

            LOOK = 1
            li_next = load_slot(0)
            for slot in range(8):
                li = li_next
                if slot + 1 < 8:
                    li_next = load_slot(slot + 1)
                units = []
                for qt in range(NT):
                    if slot < 7:
                        for kb in range(0, 16, 2):
                            units.append([(li, kb, qt, 0, False), (li, kb + 1, qt, 0, False)])
                    else:
                        for kb in range(0, 4 * qt, 2):
                            units.append([(li, kb, qt, 0, False), (li, kb + 1, qt, 0, False)])
                        for j in range(4):
                            units.append([(li, 4 * qt + j, qt, j * 128, True)])
                pend = []
                for u in units:
                    pend.append((u, emit_S(u)))
                    if len(pend) > LOOK:
                        a = pend.pop(0)
                        emit_EP(*a)
                for a in pend:
                    emit_EP(*a)
            assert all(acc_cnt[qt] == n_acc[qt] for qt in range(NT))
            at = ATT[(h // 2) % 2]
            po = (h % 2) * 64
            for qt in range(NT):
                rd = RDEN[qt % 2]
                dve(lambda e, rd=rd, qt=qt: e.reciprocal(out=rd[64:128, :], in_=PS[64:128, qt, :]),
                    [('ps', qt)], [('rden', qt % 2)])
                dve(lambda e, rd=rd, qt=qt, at=at, po=po: e.tensor_tensor(
                    out=at[po:po + 64, tsl(qt)], in0=PS[0:64, qt, :], in1=rd[64:128, :], op=ALU.mult),
                    [('ps', qt), ('rden', qt % 2)], [('att', (h // 2) % 2, h % 2, qt)])
            if h % 2 == 1:
                dump('att', at, [('att', (h // 2) % 2, i_, qt) for i_ in range(2) for qt in range(NT)], [128, T], BF16)
                pr = h // 2
                for d in range(KC):
                    for tt in range(NT):
                        mm(PS[:, 4, :], WOAT[:, pr, d * 128:(d + 1) * 128], at[:, tsl(tt)], True, True,
                           [('woat',), ('att', pr % 2, 0, tt), ('att', pr % 2, 1, tt)], [('ps', 4)])
                        dve(lambda e, d=d, tt=tt: e.tensor_tensor(out=X[:, d, tsl(tt)], in0=PS[:, 4, :],
                                                                  in1=X[:, d, tsl(tt)], op=ALU.add),
                            [('ps', 4), xk(d, tt)], [xk(d, tt)])
        P.fence()

    def conv_mixer(l):
        o_ = l // 2
        rmsnorm_to_H(C_MIX + l * 8)
        dve(lambda e: e.memset(CZ[:, 0:2], 0.0), [], [('cz', -1)])
        wcol = lambda oc, j: cv(C_CONVW + (o_ * 8 + oc) * 3 + j)
        pb = {'n': 0}
        for s in range(4):
            sl = next_slab()
            W = RING[sl].rearrange("p (k o j c) -> p k o j c", k=KC, o=2, j=3)
            for o2 in range(2):
                oc = 2 * s + o2
                for tt in range(NT):
                    bb, bc, bz = (0, 1, 2) if pb['n'] % 2 == 0 else (3, 4, 5)
                    pb['n'] += 1
                    for (bk, jj) in ((bb, 0), (bc, 1), (bz, 2)):
                        for k in range(KC):
                            mm(PS[:, bk, :], W[:, k, o2, jj, :], H[:, k, tsl(tt)], k == 0, k == KC - 1,
                               [('ring', sl), ('h', k, tt)], [('ps', bk)])
                    ct = RSTD[tt % 2]
                    act(ct, PS[:, bc, :], AF.Identity, [('ps', bc)], [('rstd', tt % 2)])
                    dve(lambda e, ct=ct, bz=bz, tt=tt: e.tensor_tensor(out=CZ[:, 2 + tt * 512:2 + (tt + 1) * 512], in0=ct,
                                                                      in1=PS[:, bz, :], op=ALU.mult),
                        [('rstd', tt % 2), ('ps', bz)], [('cz', tt)])
                    ys = YS[tt % 2]
                    czr = [('cz', tt), ('cz', tt - 1)]
                    dve(lambda e, ys=ys, tt=tt, oc=oc: e.tensor_scalar(out=ys, in0=CZ[:, 2 + tt * 512:2 + (tt + 1) * 512],
                                                                      scalar1=wcol(oc, 2), scalar2=None, op0=ALU.mult),
                        czr + [('cv',)], [('ys', tt % 2)])
                    dve(lambda e, ys=ys, tt=tt, oc=oc: e.scalar_tensor_tensor(
                        out=ys, in0=CZ[:, 1 + tt * 512:1 + (tt + 1) * 512], scalar=wcol(oc, 1), in1=ys,
                        op0=ALU.mult, op1=ALU.add), czr + [('ys', tt % 2), ('cv',)], [('ys', tt % 2)])
                    dve(lambda e, ys=ys, tt=tt, oc=oc: e.scalar_tensor_tensor(
                        out=ys, in0=CZ[:, tt * 512:(tt + 1) * 512], scalar=wcol(oc, 0), in1=ys,
                        op0=ALU.mult, op1=ALU.add), czr + [('ys', tt % 2), ('cv',)], [('ys', tt % 2)])
                    dve(lambda e, ys=ys, tt=tt, oc=oc, bb=bb: e.tensor_tensor(out=GATED[:, oc, tsl(tt)], in0=ys,
                                                                             in1=PS[:, bb, :], op=ALU.mult),
                        [('ys', tt % 2), ('ps', bb)], [('gated', oc, tt)])
                    if tt == 0:
                        act(BSAVE[:, oc, :], PS[:, bb, 0:2], AF.Identity, [('ps', bb)], [('bsave', oc)])
                        act(YSAVE[:, oc, :], ys[:, 0:2], AF.Identity, [('ys', tt % 2)], [('ysave', oc)])
                    if tt == NT - 1:
                        act(HOUT[:, oc, :], CZ[:, 2 + T - 2:2 + T], AF.Identity, [('cz', tt)], [('hout', oc)])
            after_use()
        hin = hin_d[o_].ap()
        hall = hall_d[o_].ap()
        P.add('sp', lambda e: e.dma_start(out=hin, in_=HOUT.rearrange("p o t -> p (o t)")),
              reads=[('hout', oc) for oc in range(8)], writes=[('hin', o_)], kind='d')
        P.add('pool', lambda e: e.collective_compute(
            "AllGather", ALU.bypass, replica_groups=[list(range(NCORES))],
            ins=[hin_d[o_].ap().opt()], outs=[hall_d[o_].ap().opt()]),
            reads=[('hin', o_)], writes=[('hall', o_)], kind='cc')
        P.add('sp', lambda e: e.dma_start(out=HALL, in_=hall.rearrange("(r p) f -> p r f", p=128)),
              reads=[('hall', o_)], writes=[('hallsb',)], kind='d')
        HZf = HZ.rearrange("p o t -> p (o t)")
        for r in range(8):
            if r == 0:
                dve(lambda e: e.tensor_scalar(out=HZf, in0=HALL[:, 0, :], scalar1=cv(C_HSEL), scalar2=None, op0=ALU.mult),
                    [('hallsb',), ('cv',)], [('hz',)])
            else:
                dve(lambda e, r=r: e.scalar_tensor_tensor(out=HZf, in0=HALL[:, r, :], scalar=cv(C_HSEL + r), in1=HZf,
                                                          op0=ALU.mult, op1=ALU.add),
                    [('hallsb',), ('hz',), ('cv',)], [('hz',)])
        CW = CV[:, C_CONVW + o_ * 24:C_CONVW + o_ * 24 + 24].rearrange("p (o j) -> p o j", o=8)
        dve(lambda e: e.tensor_tensor(out=FIX[:, :, 0], in0=CW[:, :, 0], in1=HZ[:, :, 0], op=ALU.mult),
            [('hz',), ('cv',)], [('fix', 0)])
        dve(lambda e: e.tensor_tensor(out=TMP8, in0=CW[:, :, 1], in1=HZ[:, :, 1], op=ALU.mult),
            [('hz',), ('cv',)], [('tmp8',)])
        dve(lambda e: e.tensor_tensor(out=FIX[:, :, 0], in0=FIX[:, :, 0], in1=TMP8, op=ALU.add),
            [('fix', 0), ('tmp8',)], [('fix', 0)])
        dve(lambda e: e.tensor_tensor(out=FIX[:, :, 1], in0=CW[:, :, 0], in1=HZ[:, :, 1], op=ALU.mult),
            [('hz',), ('cv',)], [('fix', 1)])
        dve(lambda e: e.tensor_tensor(out=FIX, in0=FIX, in1=YSAVE, op=ALU.add),
            [('fix', 0), ('fix', 1)] + [('ysave', oc) for oc in range(8)], [('fix', 0), ('fix', 1)])
        dve(lambda e: e.tensor_tensor(out=GATED[:, :, 0:2], in0=FIX, in1=BSAVE, op=ALU.mult),
            [('fix', 0), ('fix', 1)] + [('bsave', oc) for oc in range(8)], [('gated', oc, 0) for oc in range(8)])
        for s in range(2):
            sl = next_slab()
            W = RING[sl][:, 0:4096].rearrange("p (k c) -> p k c", k=KC)
            for dd in range(4):
                d = s * 4 + dd
                for tt in (1, 2, 3, 0):
                    bo = 6 + (pb['n'] % 2)
                    pb['n'] += 1
                    for k in range(KC):
                        mm(PS[:, bo, :], W[:, k, dd * 128:(dd + 1) * 128], GATED[:, k, tsl(tt)], k == 0, k == KC - 1,
                           [('ring', sl), ('gated', k, tt)], [('ps', bo)])
                    dve(lambda e, d=d, tt=tt, bo=bo: e.tensor_tensor(out=X[:, d, tsl(tt)], in0=PS[:, bo, :],
                                                                    in1=X[:, d, tsl(tt)], op=ALU.add),
                        [('ps', bo), xk(d, tt)], [xk(d, tt)])
            after_use()
        P.fence()

    done = False
    for l in range(n_layers):
        lastl = (l == n_layers - 1)
        ffn(l, 0)
        P.fence()
        if lastl and stop_step == 'a':
            break
        if l % 2 == 0:
            even_mixer(l)
        else:
            conv_mixer(l)
        if lastl and stop_step == 'b':
            break
        ffn(l, 1)
        P.fence()
    if final_norm:
        rmsnorm_to_H(C_FIN, in_place=True)
    last = None
    for k in range(KC):
        last = P.add('sp', lambda e, k=k: e.dma_start(out=out_d[:, k, :], in_=X[:, k, :]),
                     reads=[xk(k, tt) for tt in range(NT)], writes=[('out', k)], kind='d')
    P.add('sp', None, reads=[('out', k) for k in range(KC)], writes=[], kind='w')

    n_dma_sp, n_dma_pool, n_cc = 8, 6, 8
    names = ['pe', 'act', 'dve', 'pool'] + [f'dsp{i}' for i in range(n_dma_sp)] + \
            [f'dpl{i}' for i in range(n_dma_pool)] + [f'cc{i}' for i in range(n_cc)]
    guards = [nc.semaphore(n) for n in names]
    hs = [g.__enter__() for g in guards]
    sems = {'pe': hs[0], 'act': hs[1], 'dve': hs[2], 'pool': hs[3],
            'dma_sp': hs[4:4 + n_dma_sp], 'dma_pool': hs[4 + n_dma_sp:4 + n_dma_sp + n_dma_pool],
            'cc': hs[4 + n_dma_sp + n_dma_pool:]}
    P.emit(sems)
    P.dbg = dbg_seen
    return nc, P


def _f32(a):
    return np.ascontiguousarray(np.asarray(a, dtype=np.float32))


def prep_shared(inp):
    sh = {}
    wffn = np.empty((DEPTH * 2, NG, 128, SLOT), np.float32)
    for l in range(DEPTH):
        for which, pfx in ((0, "ffn_pre"), (1, "ffn_post")):
            wg = _f32(inp[pfx + "_w_gate"][l]).reshape(KC, 128, NG, 256)
            wu = _f32(inp[pfx + "_w_up"][l]).reshape(KC, 128, NG, 256)
            wd = _f32(inp[pfx + "_w_down"][l]).reshape(NG, 2, 128, D)
            li = l * 2 + which
            gu = np.concatenate([wg, wu], axis=3)
            wffn[li, :, :, 0:4096] = gu.transpose(2, 1, 0, 3).reshape(NG, 128, 4096)
            wffn[li, :, :, 4096:6144] = wd.transpose(0, 2, 1, 3).reshape(NG, 128, 2048)
    sh["wffn"] = wffn
    NQ, NKV, NR = 384, 256, 32
    we1 = np.empty((2, 128, 5120), np.float32)
    we2 = np.empty((2, 128, 2048), np.float32)
    we4 = np.empty((2, 128, 2048), np.float32)
    wuq = np.empty((2, 128, 3072), np.float32)
    weu = np.empty((2, 128, 4096), np.float32)
    wev = np.empty((2, 128, 4096), np.float32)
    wosg = np.empty((2, 128, 4096), np.float32)
    woat = np.empty((2, 128, 4096), np.float32)
    sgwT = np.empty((2, 128, 1024), np.float32)
    sgb = np.empty((2, 128, 512), np.float32)
    gsg = np.empty((2, 128, 512), np.float32)
    for e in range(2):
        win = _f32(inp["even_w_in"][e])
        w3 = win.reshape(KC, 128, 1696)
        we1[e] = w3[:, :, 0:640].transpose(1, 0, 2).reshape(128, 5120)
        kr1 = w3[:, :, 640:656]
        kr2 = w3[:, :, 656:672]
        A = np.concatenate([kr1, kr2, kr2, kr1], axis=2)
        B = np.concatenate([kr2, kr1, kr1, kr2], axis=2)
        ab = np.concatenate([A, A, B, B], axis=2)
        we2[e] = ab.transpose(1, 0, 2).reshape(128, 2048)
        weu[e] = w3[:, :, 672:1184].transpose(1, 0, 2).reshape(128, 4096)
        wev[e] = w3[:, :, 1184:1696].transpose(1, 0, 2).reshape(128, 4096)
        ukv = _f32(inp["w_ukv"][e]).reshape(2, 128, 8, 128)
        we4[e, :, 0:1024] = ukv[:, :, :, 0:64].transpose(1, 0, 2, 3).reshape(128, 1024)
        we4[e, :, 1024:2048] = ukv[:, :, :, 64:128].transpose(1, 0, 2, 3).reshape(128, 1024)
        uq = _f32(inp["w_uq"][e]).reshape(3, 128, 8, 96)
        q1 = uq[..., 64:80]
        q2 = uq[..., 80:96]
        uqx = np.concatenate([uq[..., 0:64], q1, q2, q2, q1], axis=3)
        wuq[e] = uqx.transpose(1, 0, 2, 3).reshape(128, 3072)
        wo = _f32(inp["even_w_out"][e])
        woat[e] = wo[0:512].reshape(4, 128, D).transpose(1, 0, 2).reshape(128, 4096)
        wosg[e] = wo[512:1024].reshape(4, 128, D).transpose(1, 0, 2).reshape(128, 4096)
        sw = _f32(inp["sg_w"][e])
        sgwT[e] = sw.transpose(2, 0, 1).reshape(128, 1024)
        sb_ = _f32(inp["sg_b"][e])
        t4 = sb_.reshape(4, 2, 128)
        sgb[e] = np.repeat(t4.transpose(1, 0, 2)[:, None, :, :], 64, axis=1).reshape(128, 512)
        gsg[e] = np.broadcast_to(_f32(inp["sg_norm"][e])[None, :], (128, 512))
    sh.update(we1=we1, we2=we2, we4=we4, wuq=wuq, weu=weu, wev=wev, wosg=wosg, woat=woat,
              sgwT=sgwT, sgb=sgb, gsg=gsg)
    wc1 = np.empty((2, 4, 128, SLOT), np.float32)
    wc2 = np.empty((2, 2, 128, 4096), np.float32)
    for o in range(2):
        cw = _f32(inp["conv_w_in"][o]).reshape(KC, 128, 3, 4, 2, 128)
        wc1[o] = cw.transpose(3, 1, 0, 4, 2, 5).reshape(4, 128, SLOT)
        co = _f32(inp["conv_w_out"][o]).reshape(KC, 128, 2, 512)
        wc2[o] = co.transpose(2, 1, 0, 3).reshape(2, 128, 4096)
    sh.update(wc1=wc1, wc2=wc2)
    p_ = np.arange(128)
    sh["tri"] = (p_[:, None] <= p_[None, :]).astype(np.float32)
    return sh


def prep_cvec(inp, c):
    cvv = np.zeros((128, NCV), np.float32)

    def fm(v, nk):
        return _f32(v).reshape(nk, 128).T

    for l in range(DEPTH):
        cvv[:, C_PRE + l * 8:C_PRE + l * 8 + 8] = fm(inp["ffn_pre_norm"][l], 8)
        cvv[:, C_MIX + l * 8:C_MIX + l * 8 + 8] = fm(inp["mix_norm"][l], 8)
        cvv[:, C_POST + l * 8:C_POST + l * 8 + 8] = fm(inp["ffn_post_norm"][l], 8)
    cvv[:, C_FIN:C_FIN + 8] = fm(inp["final_norm"], 8)
    for e in range(2):
        cvv[:, C_QN + e * 3:C_QN + e * 3 + 3] = fm(inp["q_norm"][e], 3)
        cvv[:, C_KVN + e * 2:C_KVN + e * 2 + 2] = fm(inp["kv_norm"][e], 2)
    for o in range(2):
        cw = _f32(inp["conv_w"][o])
        for oc in range(8):
            for j in range(3):
                cvv[:, C_CONVW + (o * 8 + oc) * 3 + j] = cw[j, oc * 128:(oc + 1) * 128]
    p_ = np.arange(128)
    inv_freq = (10000.0 ** (-(np.arange(0, 32, 2, dtype=np.float32)) / 32.0)).astype(np.float32)
    cvv[:, C_INVF] = inv_freq[p_ % 16]
    grp = (p_ - 64) // 16
    cvv[:, C_PHQ] = np.where((p_ >= 64) & (grp < 2), math.pi / 2, 0.0)
    cvv[:, C_PHQ2] = np.where((p_ >= 64) & (grp >= 2), math.pi / 2, 0.0)
    sg = np.ones(128, np.float32)
    sg[(p_ >= 64) & (grp % 2 == 0)] = -1.0
    cvv[:, C_SGNB] = sg
    for s in range(8):
        cvv[:, C_SLOT + s] = 1.0 if s < c else 0.0
        cvv[:, C_HSEL + s] = 1.0 if s == c - 1 else 0.0
    return cvv


_CACHE = {}


def run(inputs, n_layers=DEPTH, stop_step='c', final_norm=True, debug=False):
    key = (n_layers, stop_step, final_norm, debug)
    if key not in _CACHE:
        _CACHE[key] = build_program(n_layers, stop_step, final_norm, debug)[0]
    nc = _CACHE[key]
    sh = prep_shared(inputs)
    sh["wffn"] = sh["wffn"][0:n_layers * 2].reshape(n_layers * 2 * NG, 128, SLOT)
    sh["wc1"] = sh["wc1"].reshape(8, 128, SLOT)
    sh["wc2"] = sh["wc2"].reshape(4, 128, 4096)
    big = ("wffn", "we1", "we2", "we4", "wuq", "weu", "wev", "wosg", "woat", "sgwT", "wc1", "wc2")
    x = _f32(inputs["x"])[0]
    pos = np.asarray(inputs["positions"]).astype(np.int32)[0]
    in_maps = []
    for c in range(NCORES):
        xc = x[c * T:(c + 1) * T]
        m = dict(sh)
        for nm in big:
            a_ = sh[nm]
            m[nm] = np.concatenate([a_, np.full((1,) + a_.shape[1:], float(c), np.float32)], axis=0)
        m["xT"] = np.ascontiguousarray(xc.T.reshape(KC, 128, T).transpose(1, 0, 2))
        m["pos"] = np.ascontiguousarray(np.broadcast_to(pos[c * T:(c + 1) * T][None, :], (128, T)))
        m["cvec"] = prep_cvec(inputs, c)
        in_maps.append(m)
    res = run_bass_kernel_spmd(nc, in_maps, core_ids=list(range(NCORES)))
    if debug:
        _CACHE['dbg'] = [{k: np.asarray(v) for k, v in res.results[c].items() if k.startswith('dbg_')} for c in range(NCORES)]
    outs = []
    for c in range(NCORES):
        o = np.asarray(res.results[c]["outT"])
        outs.append(o.transpose(2, 1, 0).reshape(T, D))
    return np.concatenate(outs, axis=0)[None].astype(np.float32)


def kernel(**inputs):
    return run(inputs)
```
